# Optimizing a Trainium2 kernel written in Bass

```python
import jax, jax.numpy as jnp
from jax import lax
import numpy as np

D_MODEL = 2048
BATCH = 4
SEQ = 2048
DEPTH = 4
DEC_BATCH = 128
DEC_SEQ = 1
PAST_LEN = 16384
PAGE_SIZE = 128

POOL_WIDTH = D_MODEL // 4
POOL_WINDOWS = (2, 4, 8, 16)
POOL_GROUPS = len(POOL_WINDOWS)
POOL_GROUP_DIM = POOL_WIDTH // POOL_GROUPS
POOL_BUF = max(POOL_WINDOWS) - 1
LRU_WIDTH = D_MODEL // 2
LRU_BLOCKS = 8
LRU_BLOCK_DIM = LRU_WIDTH // LRU_BLOCKS
LRU_CONV = 4
LRU_C = 8.0
CONF_WIDTH = D_MODEL // 4
CONF_CONV = 31
D_FF = 4 * D_MODEL
N_BRANCH = 3
IN_COLS = POOL_WIDTH + 2 * LRU_WIDTH + 2 * CONF_WIDTH
EPS = 1e-6

kernel_name = "hybrid_pool_rglru_conformer_step"


def rms_norm(x, g):
    xf = x.astype(jnp.float32)
    y = xf * lax.rsqrt(jnp.mean(xf * xf, axis=-1, keepdims=True) + EPS)
    return (y * g.astype(jnp.float32)).astype(x.dtype)


def layer_norm(x, g, b):
    xf = x.astype(jnp.float32)
    mu = jnp.mean(xf, axis=-1, keepdims=True)
    var = jnp.mean(jnp.square(xf - mu), axis=-1, keepdims=True)
    y = (xf - mu) * lax.rsqrt(var + EPS)
    return (y * g.astype(jnp.float32) + b.astype(jnp.float32)).astype(x.dtype)


def causal_dwconv(buf, x, w, b):
    ext = jnp.concatenate([buf.astype(x.dtype), x], axis=1)
    out = lax.conv_general_dilated(
        ext, w[:, None, :].astype(x.dtype), window_strides=(1,), padding='VALID',
        dimension_numbers=('NWC', 'WIO', 'NWC'), feature_group_count=x.shape[-1])
    k1 = w.shape[0] - 1
    return out + b.astype(x.dtype), ext[:, ext.shape[1] - k1:]


def pool_mixer(u, buf, start_pos, w_grp, scale):
    B, T, P = u.shape
    ext = jnp.concatenate([buf.astype(u.dtype), u], axis=1)
    cs = jnp.cumsum(ext.astype(jnp.float32), axis=1)
    cs = jnp.concatenate([jnp.zeros((B, 1, P), jnp.float32), cs], axis=1)
    end = cs[:, POOL_BUF + 1:POOL_BUF + 1 + T]
    pos = (start_pos + jnp.arange(T)).astype(jnp.float32)
    pooled = []
    for gi, w in enumerate(POOL_WINDOWS):
        sl = slice(gi * POOL_GROUP_DIM, (gi + 1) * POOL_GROUP_DIM)
        s = end[:, :, sl] - cs[:, POOL_BUF + 1 - w:POOL_BUF + 1 - w + T, sl]
        cnt = jnp.minimum(jnp.float32(w), pos + 1.0)
        pooled.append(s / cnt[None, :, None])
    d = jnp.concatenate(pooled, axis=-1) - u.astype(jnp.float32)
    d = d.astype(u.dtype).reshape(B, T, POOL_GROUPS, POOL_GROUP_DIM)
    y = jnp.einsum('btgc,gcd->btgd', d, w_grp).reshape(B, T, P) * scale
    return y, ext[:, ext.shape[1] - POOL_BUF:]


def rg_lru(x, h0, start_pos, w_a, b_a, w_x, b_x, lam):
    B, T, R = x.shape
    xb = x.reshape(B, T, LRU_BLOCKS, LRU_BLOCK_DIM)
    gate_r = jax.nn.sigmoid((jnp.einsum('btnc,ncd->btnd', xb, w_a).reshape(B, T, R) + b_a).astype(jnp.float32))
    gate_i = jax.nn.sigmoid((jnp.einsum('btnc,ncd->btnd', xb, w_x).reshape(B, T, R) + b_x).astype(jnp.float32))
    log_a = -LRU_C * gate_r * jax.nn.softplus(-lam.astype(jnp.float32))
    reset = ((start_pos + jnp.arange(T)) == 0)[None, :, None]
    a = jnp.where(reset, 0.0, jnp.exp(log_a))
    mult = jnp.where(reset, 1.0, jnp.sqrt(-jnp.expm1(2.0 * log_a)))
    bterm = x.astype(jnp.float32) * gate_i * mult
    bterm = bterm.at[:, 0].add(a[:, 0] * h0.astype(jnp.float32))

    def combine(l, r):
        return (l[0] * r[0], r[0] * l[1] + r[1])

    _, h = lax.associative_scan(combine, (a, bterm), axis=1)
    return h, h[:, -1]


def hybrid_layer(x, pool_buf, lru_buf, lru_h, conf_buf, start_pos, W):
    (g_mix, w_in, w_pool_grp, pool_scale, w_pool_br, w_lru_conv, b_lru_conv,
     w_lru_a, b_lru_a, w_lru_x, b_lru_x, lru_lambda, w_lru_br, w_conf_conv,
     b_conf_conv, g_conf, b_conf, w_conf_br, w_gate, b_gate, w_out, g_mlp,
     w_up, w_down) = W
    xn = rms_norm(x, g_mix)
    z = xn @ w_in
    c0 = POOL_WIDTH
    c1 = c0 + LRU_WIDTH
    c2 = c1 + LRU_WIDTH
    c3 = c2 + CONF_WIDTH
    u_pool, u_lru, u_gel, c_a, c_b = z[..., :c0], z[..., c0:c1], z[..., c1:c2], z[..., c2:c3], z[..., c3:]
    y_pool, new_pool = pool_mixer(u_pool, pool_buf, start_pos, w_pool_grp, pool_scale)
    y_pool = y_pool @ w_pool_br
    xc, new_lru_buf = causal_dwconv(lru_buf, u_lru, w_lru_conv, b_lru_conv)
    h, h_last = rg_lru(xc, lru_h, start_pos, w_lru_a, b_lru_a, w_lru_x, b_lru_x, lru_lambda)
    y_lru = (h.astype(x.dtype) * jax.nn.gelu(u_gel, approximate=True)) @ w_lru_br
    v = c_a * jax.nn.sigmoid(c_b)
    vc, new_conf = causal_dwconv(conf_buf, v, w_conf_conv, b_conf_conv)
    y_conf = jax.nn.silu(layer_norm(vc, g_conf, b_conf)) @ w_conf_br
    gates = jax.nn.sigmoid(xn @ w_gate + b_gate)
    g_a, g_b, g_c = jnp.split(gates, N_BRANCH, axis=-1)
    x = x + (g_a * y_pool + g_b * y_lru + g_c * y_conf) @ w_out
    hid = jnp.square(jax.nn.relu(rms_norm(x, g_mlp) @ w_up))
    x = x + hid @ w_down
    return x, new_pool, new_lru_buf, h_last.astype(x.dtype), new_conf


def setup_inputs(seed: int = 0) -> dict:
    key = jax.random.key(seed)
    ks = jax.random.split(key, 40)
    f32 = jnp.float32

    def nrm(k, shape, scale):
        return jax.random.normal(k, shape, f32) * scale

    D, L = D_MODEL, DEPTH
    p = 1.0 / (1.0 + 0.0)
    a0 = jax.random.uniform(ks[0], (L, LRU_WIDTH), f32, 0.9, 0.999)
    s = a0 ** (1.0 / LRU_C)
    lru_lambda = jnp.log(s) - jnp.log1p(-s)
    return {
        "x_prompt": nrm(ks[1], (BATCH, SEQ, D), 1.0),
        "x_sample": nrm(ks[2], (DEC_BATCH, DEC_SEQ, D), 1.0),
        "state_pool": nrm(ks[3], (L, DEC_BATCH, POOL_BUF, POOL_WIDTH), 1.0),
        "state_lru_conv": nrm(ks[4], (L, DEC_BATCH, LRU_CONV - 1, LRU_WIDTH), 1.0),
        "state_lru_h": nrm(ks[5], (L, DEC_BATCH, LRU_WIDTH), 0.5),
        "state_conf_conv": nrm(ks[6], (L, DEC_BATCH, CONF_CONV - 1, CONF_WIDTH), 1.0),
        "g_mix": 1.0 + nrm(ks[7], (L, D), 0.05),
        "w_in": nrm(ks[8], (L, D, IN_COLS), D ** -0.5),
        "w_pool_grp": nrm(ks[9], (L, POOL_GROUPS, POOL_GROUP_DIM, POOL_GROUP_DIM), POOL_GROUP_DIM ** -0.5),
        "pool_scale": 1.0 + nrm(ks[10], (L, POOL_WIDTH), 0.1),
        "w_pool_br": nrm(ks[11], (L, POOL_WIDTH, D), POOL_WIDTH ** -0.5),
        "w_lru_conv": nrm(ks[12], (L, LRU_CONV, LRU_WIDTH), LRU_CONV ** -0.5),
        "b_lru_conv": nrm(ks[13], (L, LRU_WIDTH), 0.02),
        "w_lru_a": nrm(ks[14], (L, LRU_BLOCKS, LRU_BLOCK_DIM, LRU_BLOCK_DIM), LRU_BLOCK_DIM ** -0.5),
        "b_lru_a": nrm(ks[15], (L, LRU_WIDTH), 0.02),
        "w_lru_x": nrm(ks[16], (L, LRU_BLOCKS, LRU_BLOCK_DIM, LRU_BLOCK_DIM), LRU_BLOCK_DIM ** -0.5),
        "b_lru_x": nrm(ks[17], (L, LRU_WIDTH), 0.02),
        "lru_lambda": lru_lambda,
        "w_lru_br": nrm(ks[18], (L, LRU_WIDTH, D), LRU_WIDTH ** -0.5),
        "w_conf_conv": nrm(ks[19], (L, CONF_CONV, CONF_WIDTH), CONF_CONV ** -0.5),
        "b_conf_conv": nrm(ks[20], (L, CONF_WIDTH), 0.02),
        "g_conf": 1.0 + nrm(ks[21], (L, CONF_WIDTH), 0.05),
        "b_conf": nrm(ks[22], (L, CONF_WIDTH), 0.02),
        "w_conf_br": nrm(ks[23], (L, CONF_WIDTH, D), CONF_WIDTH ** -0.5),
        "w_gate": nrm(ks[24], (L, D, N_BRANCH * D), D ** -0.5),
        "b_gate": nrm(ks[25], (L, N_BRANCH * D), 0.02),
        "w_out": nrm(ks[26], (L, D, D), D ** -0.5),
        "g_mlp": 1.0 + nrm(ks[27], (L, D), 0.05),
        "w_up": nrm(ks[28], (L, D, D_FF), D ** -0.5),
        "w_down": nrm(ks[29], (L, D_FF, D), 0.5 * D_FF ** -0.5),
        "g_final": 1.0 + nrm(ks[30], (D,), 0.05),
    }


def reference(x_prompt, x_sample, state_pool, state_lru_conv, state_lru_h, state_conf_conv,
              g_mix, w_in, w_pool_grp, pool_scale, w_pool_br, w_lru_conv, b_lru_conv,
              w_lru_a, b_lru_a, w_lru_x, b_lru_x, lru_lambda, w_lru_br, w_conf_conv,
              b_conf_conv, g_conf, b_conf, w_conf_br, w_gate, b_gate, w_out, g_mlp,
              w_up, w_down, g_final):
    Bp = x_prompt.shape[0]
    dt = x_prompt.dtype
    xp, xs = x_prompt, x_sample
    pool_p, lconv_p, lh_p, cconv_p = [], [], [], []
    pool_s, lconv_s, lh_s, cconv_s = [], [], [], []
    for l in range(DEPTH):
        W = (g_mix[l], w_in[l], w_pool_grp[l], pool_scale[l], w_pool_br[l], w_lru_conv[l],
             b_lru_conv[l], w_lru_a[l], b_lru_a[l], w_lru_x[l], b_lru_x[l], lru_lambda[l],
             w_lru_br[l], w_conf_conv[l], b_conf_conv[l], g_conf[l], b_conf[l], w_conf_br[l],
             w_gate[l], b_gate[l], w_out[l], g_mlp[l], w_up[l], w_down[l])
        xp, a1, a2, a3, a4 = hybrid_layer(
            xp,
            jnp.zeros((Bp, POOL_BUF, POOL_WIDTH), dt),
            jnp.zeros((Bp, LRU_CONV - 1, LRU_WIDTH), dt),
            jnp.zeros((Bp, LRU_WIDTH), dt),
            jnp.zeros((Bp, CONF_CONV - 1, CONF_WIDTH), dt),
            0, W)
        pool_p.append(a1); lconv_p.append(a2); lh_p.append(a3); cconv_p.append(a4)
        xs, b1, b2, b3, b4 = hybrid_layer(
            xs, state_pool[l], state_lru_conv[l], state_lru_h[l], state_conf_conv[l],
            PAST_LEN, W)
        pool_s.append(b1); lconv_s.append(b2); lh_s.append(b3); cconv_s.append(b4)
    y_prompt = rms_norm(xp, g_final)
    y_sample = rms_norm(xs, g_final)
    return (y_prompt, y_sample,
            jnp.stack(pool_p), jnp.stack(lconv_p), jnp.stack(lh_p), jnp.stack(cconv_p),
            jnp.stack(pool_s), jnp.stack(lconv_s), jnp.stack(lh_s), jnp.stack(cconv_s))
```

```python
import os
from contextlib import ExitStack

import numpy as np
import concourse.bass as bass
import concourse.mybir as mybir
from concourse.bass_utils import run_bass_kernel_spmd

F32 = mybir.dt.float32
BF16 = mybir.dt.bfloat16
AF = mybir.ActivationFunctionType
ALU = mybir.AluOpType
AX = mybir.AxisListType

D = 2048
L = 4
SEQ = 2048
NS = 16
TW = 512
KC = 16
EPS = 1e-6
POOL_W = (2, 4, 8, 16)

PV_ITEMS = [("g_mix", 16), ("pool_scale", 4), ("b_lru_conv", 8), ("b_lru_a", 8),
            ("b_lru_x", 8), ("lru_lambda", 8), ("b_conf_conv", 4), ("g_conf", 4),
            ("b_conf", 4), ("b_gate", 48), ("g_mlp", 16), ("w_lru_conv", 32),
            ("w_conf_conv", 124)]
PV_PER_LAYER = sum(n for _, n in PV_ITEMS)
PV_COLS = PV_PER_LAYER * L + 16


def pv_col(name, l):
    off = 0
    for nm, n in PV_ITEMS:
        if nm == name:
            return l * PV_PER_LAYER + off
        off += n
    raise KeyError(name)


PV_GFINAL = PV_PER_LAYER * L


def _flat(deps):
    out = []
    for d in deps:
        if d is None:
            continue
        if isinstance(d, list):
            out.extend(_flat(d))
        else:
            out.append(d)
    return out


class Prog:
    def __init__(self, nc, es):
        self.nc = nc
        self.es = es
        self.q = {e: [] for e in ("pe", "act", "dve", "pool", "sp")}
        self.esem = {e: es.enter_context(nc.semaphore("s_" + e)) for e in ("pe", "act", "dve", "pool")}
        self.ecnt = {e: 0 for e in self.esem}
        self.dsem = {}
        self.dcnt = {}
        self.seen = {}

    def _waits(self, eng, deps):
        w = []
        for d in _flat(deps):
            sem, val = d
            key = (eng, id(sem))
            if self.seen.get(key, 0) >= val:
                continue
            self.seen[key] = val
            w.append((sem, val))
        return w

    def op(self, eng, name, deps=(), sig=True, **kw):
        w = self._waits(eng, deps)
        tok = None
        if sig:
            self.ecnt[eng] += 1
            tok = (self.esem[eng], self.ecnt[eng])
        self.q[eng].append((w, name, kw, self.esem[eng] if sig else None, 1))
        return tok

    def dma(self, eng, semname, deps=(), **kw):
        if semname not in self.dsem:
            self.dsem[semname] = self.es.enter_context(self.nc.semaphore("d_" + semname))
            self.dcnt[semname] = 0
        w = self._waits(eng, deps)
        self.dcnt[semname] += 16
        self.q[eng].append((w, "dma_start", kw, self.dsem[semname], 16))
        return (self.dsem[semname], self.dcnt[semname])

    def wait_only(self, eng, deps):
        w = self._waits(eng, deps)
        if w:
            self.q[eng].append((w, None, None, None, 0))

    def replay(self, eng, e):
        for w, name, kw, sem, inc in self.q[eng]:
            for (ws, wv) in w:
                e.wait_ge(ws, wv)
            if name is None:
                continue
            ins = getattr(e, name)(**kw)
            if sem is not None:
                ins.then_inc(sem, inc)


def build_program(depth, n_ptiles, do_sample):
    nc = bass.Bass("TRN2", target_bir_lowering=False)

    def din(name, shape):
        return nc.dram_tensor(name, list(shape), F32, kind="ExternalInput").ap()

    def dout(name, shape):
        return nc.dram_tensor(name, list(shape), F32, kind="ExternalOutput").ap()

    xp_d = din("xp", [SEQ, D])
    xs_d = din("xs", [NS, D])
    sp_d = din("st_pool", [L, NS, 15, 512])
    slc_d = din("st_lconv", [L, NS, 3, 1024])
    slh_d = din("st_lh", [L, NS, 1024])
    scc_d = din("st_cconv", [L, NS, 30, 512])
    w_in_d = din("w_in", [L, D, 3584])
    w_gate_d = din("w_gate", [L, D, 3 * D])
    w_out_d = din("w_out", [L, D, D])
    w_up_d = din("w_up", [L, D, 4 * D])
    w_down_d = din("w_down", [L, 4 * D, D])
    w_pbr_d = din("w_pool_br", [L, 512, D])
    w_lbr_d = din("w_lru_br", [L, 1024, D])
    w_cbr_d = din("w_conf_br", [L, 512, D])
    w_pg_d = din("w_pool_grp", [L, 4, 128, 128])
    w_la_d = din("w_lru_a", [L, 8, 128, 128])
    w_lx_d = din("w_lru_x", [L, 8, 128, 128])
    pvec_d = din("pvec", [PV_COLS, 128])
    cst_d = din("cst", [128, 194])

    yp_d = dout("yp", [SEQ, D])
    ys_d = dout("ys", [NS, D])
    npp_d = dout("npp", [L, 15, 512])
    nlcp_d = dout("nlcp", [L, 3, 1024])
    nlhp_d = dout("nlhp", [L, 8, 128])
    nccp_d = dout("nccp", [L, 30, 512])
    nps_d = dout("nps", [L, NS, 15, 512])
    nlcs_d = dout("nlcs", [L, NS, 3, 1024])
    nlhs_d = dout("nlhs", [L, NS, 1024])
    nccs_d = dout("nccs", [L, NS, 30, 512])

    NUNIT = 67
    wscr = [nc.dram_tensor("wscr%d" % i, [NUNIT, 128, 16 * 512], BF16).ap() for i in range(L)]

    es = ExitStack()
    with es:
        def sb(name, shape, dt):
            return es.enter_context(nc.sbuf_tensor(name, list(shape), dt))

        X = sb("X", [128, KC, TW], F32)
        XN = sb("XN", [128, KC, TW], BF16)
        BR = sb("BR", [128, KC, TW], BF16)
        MG = sb("MG", [128, KC, TW], BF16)
        NSLOT = 2
        WR = [sb("WR%d" % i, [128, 2, 16, 256], BF16) for i in range(NSLOT)]
        WG = sb("WG", [128, 4, 128], BF16)
        WA = sb("WA", [128, 8, 128], BF16)
        WX = sb("WX", [128, 8, 128], BF16)
        UPX = sb("UPX", [128, 4, 15 + TW], F32)
        ULX = sb("ULX", [128, 8, 3 + TW], F32)
        VX = sb("VX", [128, 4, 30 + TW], F32)
        GEL = MG
        NTMP = 7
        FT = sb("FT", [128, 4096 + NTMP * 512], F32)
        PT = [sb("PT%d" % i, [128, 528], F32) for i in range(2)]
        VXB = sb("VXB", [128, 4, 544], BF16)
        NDB = 2
        DB = [sb("DB%d" % i, [128, TW], BF16) for i in range(NDB)]
        RSTD = sb("RSTD", [128, TW], F32)
        PV = sb("PV", [128, PV_COLS], F32)
        CST = sb("CST", [128, 194], F32)
        ONES = sb("ONES", [128, 128], BF16)
        C1 = sb("C1", [128, L * 8], F32)
        CP = sb("CP", [128, L, 4, 15], F32)
        CL = sb("CL", [128, L, 8, 3], F32)
        CV = sb("CV", [128, L, 4, 30], F32)
        CH = sb("CH", [128, L * 8], F32)
        SCP = sb("SCP", [128, L, 4, NS], F32)
        SCL = sb("SCL", [128, L, 8, NS], F32)
        SCV = sb("SCV", [128, L, 4, NS], F32)
        SCH = sb("SCH", [128, L, 8, NS], F32)
        H0S = sb("H0S", [128, 8, NS], F32)

        PS = [es.enter_context(nc.psum_tensor("PS%d" % i, [128, 512], F32)) for i in range(8)]

        IDENT = CST[:, 0:128]
        DIAG = MG[:, 8:16, :].rearrange("p a (b m) -> p (a b) m", m=128)
        TAB = CST[:, 128:192]
        MASK = CST[:, 192:194]
        M4 = FT[:, 0:2048].rearrange("p (a n) -> p a n", a=4)
        YST = [FT[:, 0:2048], FT[:, 2048:4096]]
        OST = FT[:, 2048:3072]
        TMP = [FT[:, 4096 + i * 512: 4096 + (i + 1) * 512] for i in range(NTMP)]

        P = Prog(nc, es)

        state = dict(ps=0, tmp=0, db=0, wu=0, l=0, ul=0, pidx=0, npass=1, mcol=0)
        ps_free = [[] for _ in range(8)]
        tmp_free = [[] for _ in range(NTMP)]
        db_free = [[] for _ in range(NDB)]
        half_rd = [[[], []] for _ in range(NSLOT)]
        yst_free = [[], []]

        def palloc():
            i = state["ps"] % 8
            state["ps"] += 1
            return i

        def talloc():
            i = state["tmp"] % NTMP
            state["tmp"] += 1
            return i

        def dballoc():
            i = state["db"] % NDB
            state["db"] += 1
            return i

        wb_tok = {}
        slot_wb = [[[], []] for _ in range(NSLOT)]

        class Unit:
            pass

        def wload(src, kc):
            s = state["wu"] % NSLOT
            state["wu"] += 1
            l, ul = state["l"], state["ul"]
            state["ul"] += 1
            U = Unit()
            U.s = s
            scr = wscr[l][ul].rearrange("p (h k m) -> p h k m", h=2, k=16)
            if (l, ul) not in wb_tok:
                do_wb = (state["npass"] < 3) or (l < 2) or (state["pidx"] >= 1)
                U.toks = []
                wbs = []
                for h in range(2):
                    tok = P.dma("pool", "w%d_%d" % (s, h), deps=half_rd[s][h] + slot_wb[s][h],
                                out=WR[s][:, h, 0:kc, :], in_=src[:, :, h * 256:(h + 1) * 256])
                    U.toks.append(tok)
                    if do_wb:
                        t2 = P.dma("sp", "wb%d_%d" % (s, h), deps=[tok], out=scr[:, h, 0:kc, :],
                                   in_=WR[s][:, h, 0:kc, :])
                        wbs.append(t2)
                        slot_wb[s][h] = [t2]
                if do_wb:
                    wb_tok[(l, ul)] = wbs
            else:
                U.toks = []
                for h in range(2):
                    U.toks.append(P.dma("sp", "w%d_%d" % (s, h),
                                        deps=half_rd[s][h] + slot_wb[s][h] + [wb_tok[(l, ul)][h]],
                                        out=WR[s][:, h, 0:kc, :], in_=scr[:, h, 0:kc, :]))
            return U

        def u_lhsT(U, k, jj):
            return WR[U.s][:, jj // 2, k, (jj % 2) * 128:(jj % 2 + 1) * 128]

        def u_tok(U, jj):
            return U.toks[jj // 2]

        def u_done(U, jj, tok):
            if jj % 2 == 1:
                half_rd[U.s][jj // 2] = [tok]

        def mm_group(ps_i, n, pairs, deps, ncols=None, kdeps=None):
            tok = None
            npair = len(pairs)
            for k, (lt, rh) in enumerate(pairs):
                last = (k == npair - 1)
                d = (deps + ps_free[ps_i]) if k == 0 else []
                if kdeps is not None:
                    d = d + [kdeps[k]]
                tok = P.op("pe", "matmul", deps=d,
                           sig=last, out=PS[ps_i][:, 0:n], lhsT=lt, rhs=rh,
                           start=(k == 0), stop=last)
            return tok

        init = []
        init.append(P.dma("sp", "init", out=CST[:, :], in_=cst_d))
        init.append(P.op("dve", "memset", ap=ONES[:, :], constant=1.0))
        nblk = PV_COLS // 128
        for b in range(nblk):
            yb = b % 2
            t_ld = P.dma("sp", "pvld%d" % yb, deps=yst_free[yb], out=YST[yb][:, 0:128],
                         in_=pvec_d[b * 128:(b + 1) * 128, :])
            pi = palloc()
            t_tr = P.op("pe", "transpose", deps=[t_ld, init[0]] + ps_free[pi],
                        out=PS[pi][:, 0:128], in_=YST[yb][:, 0:128], identity=IDENT)
            t_cp = P.op("dve", "tensor_copy", deps=[t_tr], out=PV[:, b * 128:(b + 1) * 128],
                        in_=PS[pi][:, 0:128])
            ps_free[pi] = [t_cp]
            yst_free[yb] = [t_tr]
            init.append(t_cp)
        ti_ = talloc()
        tl = TMP[ti_]
        t_prev = init[-1]
        for l in range(L):
            c0 = pv_col("lru_lambda", l)
            t_e = P.op("act", "activation", deps=init, out=tl[:, l * 8:(l + 1) * 8],
                       in_=PV[:, c0:c0 + 8], func=AF.Exp, scale=-1.0)
            t_l = P.op("act", "activation", deps=[t_e], out=tl[:, l * 8:(l + 1) * 8],
                       in_=tl[:, l * 8:(l + 1) * 8], func=AF.Ln, bias=1.0)
            t_c = P.op("dve", "tensor_scalar", deps=[t_l], out=C1[:, l * 8:(l + 1) * 8],
                       in0=tl[:, l * 8:(l + 1) * 8], scalar1=-8.0, scalar2=None, op0=ALU.mult)
            init.append(t_c)
        tmp_free[ti_] = [init[-1]]
        for e in ("pe", "act", "dve", "pool", "sp"):
            P.wait_only(e, init)

        xtok = [None] * KC
        xrd = [[] for _ in range(KC)]
        xn_rd = []
        br_rd = [[] for _ in range(KC)]
        mg_rd = []
        ext_rd = {"UPX": [], "ULX": [], "VX": []}
        smallw_rd = []
        diag_rd = [[] for _ in range(32)]
        pt_free = [[], []]
        vxb_rd = []
        carry_tok = {}
        out_toks = []

        def pvc(name, l, i):
            c = pv_col(name, l) + i
            return PV[:, c:c + 1]

        def rmsnorm(N, gcol):
            nonlocal xn_rd
            sq = []
            for k in range(KC):
                t = P.op("act", "activation", deps=[xtok[k]] + xn_rd, out=XN[:, k, 0:N],
                         in_=X[:, k, 0:N], func=AF.Square)
                xrd[k].append(t)
                sq.append(t)
            pi = palloc()
            t_mm = mm_group(pi, N, [(ONES[:, :], XN[:, k, 0:N]) for k in range(KC)], sq)
            ti = talloc()
            t_rt = P.op("act", "activation", deps=[t_mm] + tmp_free[ti], out=TMP[ti][:, 0:N],
                        in_=PS[pi][:, 0:N], func=AF.Sqrt, scale=1.0 / D, bias=EPS)
            ps_free[pi] = [t_rt]
            t_rs0 = P.op("dve", "reciprocal", deps=[t_rt], out=RSTD[:, 0:N], in_=TMP[ti][:, 0:N])
            tmp_free[ti] = [t_rs0]
            mc = state["mcol"]
            t_rs = P.op("dve", "tensor_scalar", deps=[t_rs0], out=RSTD[:, 0:N], in0=RSTD[:, 0:N],
                        scalar1=MASK[:, mc:mc + 1], scalar2=None, op0=ALU.mult)
            xn_tok = []
            for k in range(KC):
                t = P.op("dve", "scalar_tensor_tensor", deps=[t_rs, t_mm, xtok[k]],
                         out=XN[:, k, 0:N], in0=X[:, k, 0:N], scalar=PV[:, gcol + k:gcol + k + 1],
                         in1=RSTD[:, 0:N], op0=ALU.mult, op1=ALU.mult)
                xrd[k].append(t)
                xn_tok.append(t)
            return xn_tok, t_rs

        def layer(l, N, mode, ti_idx):
            nonlocal xn_rd, mg_rd, smallw_rd
            prompt = (mode == "p")
            HP, HL, HV = 15, 3, 30

            def cur(buf, c, H):
                if prompt:
                    return buf[:, c, H:H + N]
                return buf[:, c, NS * H:NS * H + NS]

            def tap(buf, c, H, k):
                if prompt:
                    return buf[:, c, k:k + N]
                if k == H:
                    return buf[:, c, NS * H:NS * H + NS]
                return buf[:, c, 0:NS * H].rearrange("p (b j) -> p b j", j=H)[:, :, k]

            sw = [P.dma("pool", "smallw", deps=smallw_rd, out=WG[:, :, :],
                        in_=w_pg_d[l].rearrange("g c d -> c g d")),
                  P.dma("pool", "smallw", deps=smallw_rd, out=WA[:, :, :],
                        in_=w_la_d[l].rearrange("g c d -> c g d"))]
            t_sw = P.dma("pool", "smallw", deps=smallw_rd, out=WX[:, :, :],
                         in_=w_lx_d[l].rearrange("g c d -> c g d"))
            sw_tok = [t_sw]

            hist_tok = {}
            if prompt:
                for name, buf, H, nch, carry in (("UPX", UPX, HP, 4, CP), ("ULX", ULX, HL, 8, CL),
                                                 ("VX", VX, HV, 4, CV)):
                    if ti_idx == 0:
                        t = P.op("dve", "memset", deps=ext_rd[name], ap=buf[:, :, 0:H], constant=0.0)
                    else:
                        t = P.op("dve", "tensor_copy", deps=ext_rd[name] + [carry_tok[(name, l)]],
                                 out=buf[:, :, 0:H], in_=carry[:, l, :, :])
                    hist_tok[name] = t
            else:
                ld = []
                ld.append(P.dma("sp", "sst", deps=yst_free[0] + yst_free[1], out=YST[0][:, 0:512],
                                in_=sp_d[l].rearrange("b j c -> (b j) c")[0:128, :]))
                ld.append(P.dma("sp", "sst", out=YST[0][0:112, 512:1024],
                                in_=sp_d[l].rearrange("b j c -> (b j) c")[128:240, :]))
                for b4 in range(4):
                    nr = 128 if b4 < 3 else 96
                    dst = YST[0][0:nr, 1024 + b4 * 512: 1536 + b4 * 512] if b4 < 2 else \
                        YST[1][0:nr, (b4 - 2) * 512:(b4 - 1) * 512]
                    ld.append(P.dma("sp", "sst", out=dst,
                                    in_=scc_d[l].rearrange("b j c -> (b j) c")[b4 * 128:b4 * 128 + nr, :]))
                ld.append(P.dma("sp", "sst", out=YST[1][0:48, 1024:2048],
                                in_=slc_d[l].rearrange("b j c -> (b j) c")))
                t_ld = P.dma("sp", "sst", out=YST[1][64:80, 1024:2048], in_=slh_d[l])
                ld_tok = [t_ld]

                def tr_in(src_ap, nr, p0, dst_ap, deps_dst):
                    pi = palloc()
                    t_tr = P.op("pe", "transpose", deps=ld_tok + ps_free[pi], out=PS[pi][:, 0:nr],
                                in_=src_ap, identity=CST[p0:p0 + nr, p0:p0 + nr])
                    t_cp = P.op("act", "activation", deps=[t_tr] + deps_dst, out=dst_ap,
                                in_=PS[pi][:, 0:nr], func=AF.Copy)
                    ps_free[pi] = [t_cp]
                    return t_tr, t_cp

                trs = []
                cps = {"UPX": [], "ULX": [], "VX": [], "H0": []}
                for c in range(4):
                    for rb, (r0, nr) in enumerate(((0, 128), (128, 112))):
                        a, b_ = tr_in(YST[0][0:nr, rb * 512 + c * 128: rb * 512 + (c + 1) * 128], nr, 0,
                                      UPX[:, c, r0:r0 + nr], ext_rd["UPX"])
                        trs.append(a); cps["UPX"].append(b_)
                    for b4 in range(4):
                        nr = 128 if b4 < 3 else 96
                        src = YST[0][0:nr, 1024 + b4 * 512 + c * 128: 1024 + b4 * 512 + (c + 1) * 128] if b4 < 2 \
                            else YST[1][0:nr, (b4 - 2) * 512 + c * 128:(b4 - 2) * 512 + (c + 1) * 128]
                        a, b_ = tr_in(src, nr, 0, VX[:, c, b4 * 128:b4 * 128 + nr], ext_rd["VX"])
                        trs.append(a); cps["VX"].append(b_)
                for c in range(8):
                    a, b_ = tr_in(YST[1][0:48, 1024 + c * 128:1024 + (c + 1) * 128], 48, 0,
                                  ULX[:, c, 0:48], ext_rd["ULX"])
                    trs.append(a); cps["ULX"].append(b_)
                    a, b_ = tr_in(YST[1][64:80, 1024 + c * 128:1024 + (c + 1) * 128], 16, 64,
                                  H0S[:, c, :], [])
                    trs.append(a); cps["H0"].append(b_)
                yst_free[0] = [trs[-1]]
                yst_free[1] = [trs[-1]]
                hist_tok = {"UPX": cps["UPX"], "ULX": cps["ULX"], "VX": cps["VX"], "H0": cps["H0"]}
                out_toks.append(P.dma("sp", "outs", out=nps_d[l][:, 0:14, :], in_=sp_d[l][:, 1:15, :]))
                out_toks.append(P.dma("sp", "outs", out=nlcs_d[l][:, 0:2, :], in_=slc_d[l][:, 1:3, :]))
                out_toks.append(P.dma("sp", "outs", out=nccs_d[l][:, 0:29, :], in_=scc_d[l][:, 1:30, :]))

            xn_tok, _ = rmsnorm(N, pv_col("g_mix", l))

            z_tok = {}

            def hl(name):
                h = hist_tok[name]
                return h if isinstance(h, list) else [h]

            def mix_pool():
                pe_small = None
                for c in range(4):
                    w = POOL_W[c]
                    di = dballoc()
                    if prompt:
                        Lx = HP + N
                        src = UPX[:, c, :]
                        lo = HP - (w - 2)
                        d0 = [z_tok[("up", c)], hist_tok["UPX"]]
                        tprev = P.op("dve", "tensor_tensor", deps=d0 + pt_free[0] + pt_free[1], out=PT[0][:, lo:Lx], in0=src[:, lo:Lx],
                                     in1=src[:, lo - 1:Lx - 1], op=ALU.add)
                        cur_i, step = 0, 2
                        while step < w:
                            lo += step
                            tprev = P.op("dve", "tensor_tensor", deps=[tprev], out=PT[1 - cur_i][:, lo:Lx],
                                         in0=PT[cur_i][:, lo:Lx], in1=PT[cur_i][:, lo - step:Lx - step], op=ALU.add)
                            cur_i = 1 - cur_i
                            step *= 2
                        S = PT[cur_i]
                        if ti_idx == 0:
                            tprev = P.op("dve", "tensor_tensor", deps=[tprev], out=S[:, HP:HP + 16],
                                         in0=S[:, HP:HP + 16], in1=TAB[:, c * 16:(c + 1) * 16], op=ALU.mult)
                        t_d = P.op("dve", "scalar_tensor_tensor", deps=[tprev] + db_free[di], out=DB[di][:, 0:N],
                                   in0=S[:, HP:HP + N], scalar=1.0 / w, in1=src[:, HP:HP + N],
                                   op0=ALU.mult, op1=ALU.subtract)
                        pt_free[0] = [t_d]
                        pt_free[1] = [t_d]
                    else:
                        hist = UPX[:, c, 0:NS * HP].rearrange("p (b j) -> p b j", j=HP)
                        cu = UPX[:, c, NS * HP:NS * HP + NS]
                        ta = talloc()
                        d0 = [z_tok[("up", c)]] + hist_tok["UPX"] + tmp_free[ta]
                        t1 = P.op("dve", "tensor_reduce", deps=d0, out=TMP[ta][:, 0:NS], in_=hist[:, :, 16 - w:15],
                                  axis=AX.X, op=ALU.add)
                        t2 = P.op("dve", "tensor_tensor", deps=[t1], out=TMP[ta][:, 0:NS], in0=TMP[ta][:, 0:NS],
                                  in1=cu, op=ALU.add)
                        t_d = P.op("dve", "scalar_tensor_tensor", deps=[t2] + db_free[di], out=DB[di][:, 0:N],
                                   in0=TMP[ta][:, 0:NS], scalar=1.0 / w, in1=cu, op0=ALU.mult, op1=ALU.subtract)
                        tmp_free[ta] = [t_d]
                    pi = palloc()
                    t_mm = mm_group(pi, N, [(WG[:, c, :], DB[di][:, 0:N])], [t_d] + sw_tok)
                    db_free[di] = [t_mm]
                    pe_small = t_mm
                    t_e = P.op("dve", "tensor_scalar", deps=[t_mm] + br_rd[c], out=BR[:, c, 0:N], in0=PS[pi][:, 0:N],
                               scalar1=pvc("pool_scale", l, c), scalar2=None, op0=ALU.mult)
                    ps_free[pi] = [t_e]
                    z_tok[("br", c)] = t_e

                return pe_small

            def mix_lru_a(chunks):
                wl0 = pv_col("w_lru_conv", l)
                st = {}
                for c in chunks:
                    d0 = [z_tok[("ul", c)]] + hl("ULX")
                    i1, i2, i3 = talloc(), talloc(), talloc()
                    T1, T2, T3 = (TMP[i][:, 0:N] for i in (i1, i2, i3))
                    t = P.op("dve", "tensor_scalar", deps=d0 + tmp_free[i1], out=T1, in0=tap(ULX, c, HL, 0),
                             scalar1=PV[:, wl0 + c:wl0 + c + 1], scalar2=pvc("b_lru_conv", l, c),
                             op0=ALU.mult, op1=ALU.add)
                    for k in range(1, 4):
                        t = P.op("dve", "scalar_tensor_tensor", deps=[t], out=T1, in0=tap(ULX, c, HL, k),
                                 scalar=PV[:, wl0 + k * 8 + c:wl0 + k * 8 + c + 1], in1=T1, op0=ALU.mult, op1=ALU.add)
                    t_xc = t
                    di = dballoc()
                    t_cb = P.op("act", "activation", deps=[t_xc] + db_free[di], out=DB[di][:, 0:N], in_=T1, func=AF.Copy)
                    st[c] = (i1, i2, i3, T1, T2, T3, t_xc, t_cb, di)
                return st

            def mix_lru_bc(st):
                pe_small = None
                chunks = list(st.keys())
                for c in chunks:
                    (i1, i2, i3, T1, T2, T3, t_xc, t_cb, di) = st[c]
                    pa, px = palloc(), palloc()
                    t_ma = mm_group(pa, N, [(WA[:, c, :], DB[di][:, 0:N])], [t_cb] + sw_tok)
                    t_mx = mm_group(px, N, [(WX[:, c, :], DB[di][:, 0:N])], [t_cb] + sw_tok)
                    db_free[di] = [t_mx]
                    pe_small = t_mx
                    st[c] = (i1, i2, i3, T1, T2, T3, t_xc, t_cb, pa, px, t_ma, t_mx)
                st2 = {}
                for c in chunks:
                    (i1, i2, i3, T1, T2, T3, t_xc, t_cb, pa, px, t_ma, t_mx) = st[c]
                    t_gr = P.op("act", "activation", deps=[t_ma] + tmp_free[i2], out=T2, in_=PS[pa][:, 0:N],
                                func=AF.Sigmoid, bias=pvc("b_lru_a", l, c))
                    ps_free[pa] = [t_gr]
                    t_gi = P.op("act", "activation", deps=[t_mx] + tmp_free[i3], out=T3, in_=PS[px][:, 0:N],
                                func=AF.Sigmoid, bias=pvc("b_lru_x", l, c))
                    ps_free[px] = [t_gi]
                    t_a = P.op("act", "activation", deps=[t_gr], out=T2, in_=T2, func=AF.Exp,
                               scale=C1[:, l * 8 + c:l * 8 + c + 1])
                    if prompt and ti_idx == 0:
                        t_a = P.op("dve", "memset", deps=[t_a], ap=T2[:, 0:1], constant=0.0)
                    st2[c] = (t_gi, t_a)
                for c in chunks:
                    (i1, i2, i3, T1, T2, T3, t_xc, t_cb, pa, px, t_ma, t_mx) = st[c]
                    t_gi, t_a = st2[c]
                    t_b1 = P.op("dve", "tensor_tensor", deps=[t_gi, t_xc, t_cb], out=T3, in0=T3, in1=T1, op=ALU.mult)
                    t_sq = P.op("act", "activation", deps=[t_a, t_b1, t_cb], out=T1, in_=T2, func=AF.Square)
                    t_mu = P.op("act", "activation", deps=[t_sq], out=T1, in_=T1, func=AF.Sqrt, scale=-1.0, bias=1.0)
                    t_b2 = P.op("dve", "tensor_tensor", deps=[t_b1, t_mu], out=T3, in0=T3, in1=T1, op=ALU.mult)
                    if prompt:
                        if ti_idx == 0:
                            init_h = 0.0
                            dh = []
                        else:
                            init_h = CH[:, l * 8 + c:l * 8 + c + 1]
                            dh = [carry_tok[("h", l, c)]]
                        t_h = P.op("dve", "tensor_tensor_scan", deps=[t_b2, t_a] + dh, out=T1,
                                   data0=T2, data1=T3, initial=init_h, op0=ALU.mult, op1=ALU.add)
                        t_hx = t_h
                        t_c = P.op("dve", "tensor_copy", deps=[t_h], out=CH[:, l * 8 + c:l * 8 + c + 1],
                                   in_=T1[:, N - 1:N])
                        carry_tok[("h", l, c)] = t_c
                    else:
                        t_hx = P.op("dve", "tensor_tensor", deps=[t_a, t_b2] + hl("H0"), out=T1,
                                    in0=T2, in1=H0S[:, c, :], op=ALU.mult)
                        t_h = P.op("dve", "tensor_tensor", deps=[t_hx, t_b2], out=T1, in0=T1, in1=T3, op=ALU.add)
                        t_c = P.op("dve", "tensor_copy", deps=[t_h], out=SCH[:, l, c, :], in_=T1)
                        carry_tok[("sh", l, c)] = t_c
                    t_o = P.op("dve", "tensor_tensor", deps=[t_h, z_tok[("gel", c)]] + br_rd[4 + c],
                               out=BR[:, 4 + c, 0:N], in0=T1, in1=GEL[:, c, 0:N], op=ALU.mult)
                    z_tok[("br", 4 + c)] = t_o
                    tmp_free[i1] = [t_o, t_c]
                    tmp_free[i2] = [t_hx, t_h]
                    tmp_free[i3] = [t_h]
                return pe_small

            def tapb(c, k):
                if prompt:
                    return VXB[:, c, k:k + N]
                if k == HV:
                    return VXB[:, c, NS * HV:NS * HV + NS]
                return VXB[:, c, 0:NS * HV].rearrange("p (b j) -> p b j", j=HV)[:, :, k]

            wc0 = pv_col("w_conf_conv", l)
            diag_tok = {}

            def diag_build(k, c):
                slot = (k * 4 + c) % 32
                col = wc0 + k * 4 + c
                deps = diag_rd[slot] + (mg_rd if k < 8 else [])
                if k % 3 == 2:
                    return P.op("act", "activation", deps=deps, out=DIAG[:, slot, :], in_=IDENT, func=AF.Copy,
                                scale=PV[:, col:col + 1])
                return P.op("pool" if k % 3 == 0 else "dve", "tensor_scalar", deps=deps,
                            out=DIAG[:, slot, :], in0=IDENT, scalar1=PV[:, col:col + 1], scalar2=1.0,
                            op0=ALU.mult, op1=ALU.mult)

            def conf_early():
                for k in range(8):
                    for c in range(4):
                        diag_tok[(k, c)] = diag_build(k, c)

            def conf_prep():
                width = (HV + N) if prompt else (NS * HV + NS)
                toks = []
                for c in range(4):
                    toks.append(P.op("act", "activation", deps=[z_tok[("v", c)]] + hl("VX") + vxb_rd,
                                     out=VXB[:, c, 0:width], in_=VX[:, c, 0:width], func=AF.Copy))
                return toks

            def mix_conf(vxb_tok):
                pcv = [palloc() for _ in range(4)]
                t_cv = [None] * 4
                for k in range(31):
                    for c in range(4):
                        slot = (k * 4 + c) % 32
                        t_d = diag_tok[(k, c)] if k < 8 else diag_build(k, c)
                        t_m = P.op("pe", "matmul", deps=[t_d, vxb_tok[c]] + (ps_free[pcv[c]] if k == 0 else []),
                                   sig=True, out=PS[pcv[c]][:, 0:N], lhsT=DIAG[:, slot, :], rhs=tapb(c, k),
                                   start=(k == 0), stop=(k == 30))
                        diag_rd[slot] = [t_m]
                        t_cv[c] = t_m
                vxb_rd[:] = [t_cv[3]]
                p_sum, p_sq = palloc(), palloc()
                vc_tok = []
                t_ms = t_mq = None
                for c in range(4):
                    VC = M4[:, c, 0:N]
                    t = P.op("act", "activation", deps=[t_cv[c]] + yst_free[0], out=VC, in_=PS[pcv[c]][:, 0:N],
                             func=AF.Identity, bias=pvc("b_conf_conv", l, c))
                    ps_free[pcv[c]] = [t]
                    vc_tok.append(t)
                    d1, d2 = dballoc(), dballoc()
                    t_b = P.op("act", "activation", deps=[t] + db_free[d1], out=DB[d1][:, 0:N], in_=VC, func=AF.Copy)
                    t_q = P.op("act", "activation", deps=[t] + db_free[d2], out=DB[d2][:, 0:N], in_=VC, func=AF.Square)
                    t_ms = P.op("pe", "matmul", deps=[t_b] + (ps_free[p_sum] if c == 0 else []), sig=True,
                                out=PS[p_sum][:, 0:N], lhsT=ONES[:, :], rhs=DB[d1][:, 0:N], start=(c == 0), stop=(c == 3))
                    t_mq = P.op("pe", "matmul", deps=[t_q] + (ps_free[p_sq] if c == 0 else []), sig=True,
                                out=PS[p_sq][:, 0:N], lhsT=ONES[:, :], rhs=DB[d2][:, 0:N], start=(c == 0), stop=(c == 3))
                    db_free[d1] = [t_ms]
                    db_free[d2] = [t_mq]
                tm, tq, tr_ = talloc(), talloc(), talloc()
                MEAN, MSQ, RS = TMP[tm][:, 0:N], TMP[tq][:, 0:N], TMP[tr_][:, 0:N]
                t_mean = P.op("act", "activation", deps=[t_ms] + tmp_free[tm], out=MEAN, in_=PS[p_sum][:, 0:N],
                              func=AF.Copy, scale=1.0 / 512)
                ps_free[p_sum] = [t_mean]
                t_msq = P.op("act", "activation", deps=[t_mean] + tmp_free[tq], out=MSQ, in_=MEAN, func=AF.Square)
                t_var = P.op("dve", "scalar_tensor_tensor", deps=[t_mq, t_msq] + tmp_free[tr_], out=RS,
                             in0=PS[p_sq][:, 0:N], scalar=1.0 / 512, in1=MSQ, op0=ALU.mult, op1=ALU.subtract)
                ps_free[p_sq] = [t_var]
                t_sd = P.op("act", "activation", deps=[t_var], out=RS, in_=RS, func=AF.Sqrt, bias=EPS)
                t_rs = P.op("dve", "reciprocal", deps=[t_sd], out=RS, in_=RS)
                last = None
                for c in range(4):
                    VC = M4[:, c, 0:N]
                    t1 = P.op("dve", "tensor_tensor", deps=[vc_tok[c], t_mean, t_mq, t_ms], out=VC, in0=VC, in1=MEAN,
                              op=ALU.subtract)
                    t2 = P.op("dve", "tensor_tensor", deps=[t1, t_rs], out=VC, in0=VC, in1=RS, op=ALU.mult)
                    t3 = P.op("act", "activation", deps=[t2] + br_rd[12 + c], out=BR[:, 12 + c, 0:N], in_=VC,
                              func=AF.Silu, scale=pvc("g_conf", l, c), bias=pvc("b_conf", l, c))
                    z_tok[("br", 12 + c)] = t3
                    last = t3
                tmp_free[tm] = [last]
                tmp_free[tq] = [t_var]
                tmp_free[tr_] = [last]
                return last, t_cv[3]

            def save_carry(name):
                buf, H, nch, carry, sdst, key = {"UPX": (UPX, HP, 4, CP, SCP, "up"), "ULX": (ULX, HL, 8, CL, SCL, "ul"),
                                                 "VX": (VX, HV, 4, CV, SCV, "v")}[name]
                if prompt:
                    t = P.op("dve", "tensor_copy", deps=[z_tok[(key, c)] for c in range(nch)] + hl(name),
                             out=carry[:, l, :, :], in_=buf[:, :, N:N + H])
                    carry_tok[(name, l)] = t
                else:
                    t = P.op("dve", "tensor_copy", deps=[z_tok[(key, c)] for c in range(nch)],
                             out=sdst[:, l, :, :], in_=buf[:, :, NS * H:NS * H + NS])
                    carry_tok[("s" + name, l)] = t
                ext_rd[name] = [t]

            conf_early()
            last_pe = None
            lru_pending = None
            for u in (1, 2, 3, 4, 5, 6, 0):
                U = wload(w_in_d[l][:, u * 512:(u + 1) * 512].rearrange("(k p) m -> p k m", p=128), 16)
                for jj in range(4):
                    zc = u * 4 + jj
                    pi = palloc()
                    t_mm = mm_group(pi, N, [(u_lhsT(U, k, jj), XN[:, k, 0:N])
                                            for k in range(KC)], [u_tok(U, jj)], kdeps=xn_tok)
                    u_done(U, jj, t_mm)
                    last_pe = t_mm
                    if zc < 4:
                        c = zc
                        t = P.op("act", "activation", deps=[t_mm, hist_tok["UPX"]] + ext_rd["UPX"],
                                 out=cur(UPX, c, HP), in_=PS[pi][:, 0:N], func=AF.Copy)
                        ps_free[pi] = [t]
                        z_tok[("up", c)] = t
                    elif zc < 12:
                        c = zc - 4
                        t = P.op("act", "activation", deps=[t_mm, hist_tok["ULX"]] + ext_rd["ULX"],
                                 out=cur(ULX, c, HL), in_=PS[pi][:, 0:N], func=AF.Copy)
                        ps_free[pi] = [t]
                        z_tok[("ul", c)] = t
                    elif zc < 20:
                        c = zc - 12
                        t6 = P.op("act", "activation", deps=[t_mm] + mg_rd, out=GEL[:, c, 0:N],
                                  in_=PS[pi][:, 0:N], func=AF.Gelu_apprx_tanh)
                        ps_free[pi] = [t6]
                        z_tok[("gel", c)] = t6
                    elif zc < 24:
                        c = zc - 20
                        t = P.op("act", "activation", deps=[t_mm, hist_tok["VX"]] + ext_rd["VX"],
                                 out=cur(VX, c, HV), in_=PS[pi][:, 0:N], func=AF.Copy)
                        ps_free[pi] = [t]
                        z_tok[("ca", c)] = t
                    else:
                        c = zc - 24
                        A_ = PT[c % 2][:, 0:N]
                        t1 = P.op("act", "activation", deps=[t_mm] + pt_free[c % 2], out=A_,
                                  in_=PS[pi][:, 0:N], func=AF.Sigmoid)
                        ps_free[pi] = [t1]
                        t2 = P.op("dve", "tensor_tensor", deps=[t1, z_tok[("ca", c)]], out=cur(VX, c, HV),
                                  in0=cur(VX, c, HV), in1=A_, op=ALU.mult)
                        pt_free[c % 2] = [t2]
                        z_tok[("v", c)] = t2
                if u == 2:
                    save_carry("ULX")
                if lru_pending is not None:
                    pe_s1 = mix_lru_bc(lru_pending)
                    lru_pending = None
                if u in (3, 4, 5, 6):
                    lru_pending = mix_lru_a(range((u - 3) * 2, (u - 3) * 2 + 2))
                if u == 6:
                    save_carry("VX")
                    vxb_tok = conf_prep()
                if u == 0:
                    save_carry("UPX")
                    pe_s2 = mix_pool()
            xn_rd = [last_pe]

            conf_last, diag_last = mix_conf(vxb_tok)
            m4_tok = [conf_last] * 4
            smallw_rd = [pe_s2]

            BRK = (4, 8, 4)
            BRO = (0, 4, 12)
            wbr = (w_pbr_d, w_lbr_d, w_cbr_d)
            bg0 = pv_col("b_gate", l)
            mg_tok = [None] * KC
            last_pe = None
            for jg in range(4):
                for b in range(3):
                    kb = BRK[b]
                    U2 = wload(w_gate_d[l][:, b * D + jg * 512: b * D + (jg + 1) * 512]
                               .rearrange("(k p) m -> p k m", p=128), 16)
                    U1 = wload(wbr[b][l][:, jg * 512:(jg + 1) * 512].rearrange("(k p) m -> p k m", p=128), kb)
                    sgs = []
                    for jj in range(4):
                        j = jg * 4 + jj
                        pg = palloc()
                        t_g = mm_group(pg, N, [(u_lhsT(U2, k, jj), XN[:, k, 0:N])
                                               for k in range(KC)], [u_tok(U2, jj)] + xn_tok)
                        u_done(U2, jj, t_g)
                        last_pe = t_g
                        ta = talloc()
                        SG = TMP[ta][:, 0:N]
                        t_s = P.op("act", "activation", deps=[t_g] + tmp_free[ta], out=SG, in_=PS[pg][:, 0:N],
                                   func=AF.Sigmoid, bias=PV[:, bg0 + b * 16 + j:bg0 + b * 16 + j + 1])
                        ps_free[pg] = [t_s]
                        sgs.append((ta, SG, t_s))
                    for jj in range(4):
                        j = jg * 4 + jj
                        ta, SG, t_s = sgs[jj]
                        py = palloc()
                        t_y = mm_group(py, N, [(u_lhsT(U1, k, jj), BR[:, BRO[b] + k, 0:N])
                                               for k in range(kb)],
                                       [u_tok(U1, jj)] + [z_tok[("br", BRO[b] + k)] for k in range(kb)])
                        u_done(U1, jj, t_y)
                        last_pe = t_y
                        if b == 0:
                            t_m = P.op("dve", "tensor_tensor", deps=[t_s, t_y, m4_tok[jj]], out=M4[:, jj, 0:N],
                                       in0=PS[py][:, 0:N], in1=SG, op=ALU.mult)
                            ps_free[py] = [t_m]
                            tmp_free[ta] = [t_m]
                            m4_tok[jj] = t_m
                        else:
                            t_p = P.op("dve", "tensor_tensor", deps=[t_s, t_y], out=SG, in0=PS[py][:, 0:N], in1=SG,
                                       op=ALU.mult)
                            ps_free[py] = [t_p]
                            if b == 1:
                                t_m = P.op("dve", "tensor_tensor", deps=[t_p, m4_tok[jj]], out=M4[:, jj, 0:N],
                                           in0=M4[:, jj, 0:N], in1=SG, op=ALU.add)
                                m4_tok[jj] = t_m
                            else:
                                t_m = P.op("dve", "tensor_tensor", deps=[t_p, m4_tok[jj], diag_last] + mg_rd,
                                           out=MG[:, j, 0:N], in0=M4[:, jj, 0:N], in1=SG, op=ALU.add)
                                m4_tok[jj] = t_m
                                mg_tok[j] = t_m
                            tmp_free[ta] = [t_m]
            xn_rd = [last_pe]
            for k in range(KC):
                br_rd[k] = [last_pe]
            yst_free[0] = list(m4_tok)

            for jg in range(4):
                U = wload(w_out_d[l][:, jg * 512:(jg + 1) * 512].rearrange("(k p) m -> p k m", p=128), 16)
                for jj in range(4):
                    j = jg * 4 + jj
                    pi = palloc()
                    t_mm = mm_group(pi, N, [(u_lhsT(U, k, jj), MG[:, k, 0:N]) for k in range(KC)],
                                    [u_tok(U, jj)] + mg_tok)
                    u_done(U, jj, t_mm)
                    last_pe = t_mm
                    t = P.op("dve", "tensor_tensor", deps=[t_mm, xtok[j]] + xrd[j], out=X[:, j, 0:N],
                             in0=PS[pi][:, 0:N], in1=X[:, j, 0:N], op=ALU.add)
                    ps_free[pi] = [t]
                    xtok[j] = t
                    xrd[j] = []
            mg_rd = [last_pe]

            xn_tok, _ = rmsnorm(N, pv_col("g_mlp", l))
            for q in range(4):
                hid_tok = [None] * KC
                for i in range(4):
                    U = wload(w_up_d[l][:, q * D + i * 512: q * D + (i + 1) * 512]
                              .rearrange("(k p) m -> p k m", p=128), 16)
                    for jj in range(4):
                        hc = i * 4 + jj
                        pi = palloc()
                        t_mm = mm_group(pi, N, [(u_lhsT(U, k, jj), XN[:, k, 0:N])
                                                for k in range(KC)], [u_tok(U, jj)], kdeps=xn_tok)
                        u_done(U, jj, t_mm)
                        last_pe = t_mm
                        ta = talloc()
                        R = TMP[ta][:, 0:N]
                        t_r = P.op("act", "activation", deps=[t_mm] + tmp_free[ta], out=R, in_=PS[pi][:, 0:N],
                                   func=AF.Relu)
                        ps_free[pi] = [t_r]
                        t_h = P.op("dve", "tensor_tensor", deps=[t_r] + br_rd[hc], out=BR[:, hc, 0:N], in0=R, in1=R,
                                   op=ALU.mult)
                        tmp_free[ta] = [t_h]
                        hid_tok[hc] = t_h
                xn_rd = [last_pe]
                for jg in range(4):
                    U = wload(w_down_d[l][q * D:(q + 1) * D, jg * 512:(jg + 1) * 512]
                              .rearrange("(k p) m -> p k m", p=128), 16)
                    for jj in range(4):
                        j = jg * 4 + jj
                        pi = palloc()
                        t_mm = mm_group(pi, N, [(u_lhsT(U, k, jj), BR[:, k, 0:N])
                                                for k in range(KC)], [u_tok(U, jj)] + hid_tok)
                        u_done(U, jj, t_mm)
                        last_pe = t_mm
                        t = P.op("dve", "tensor_tensor", deps=[t_mm, xtok[j]] + xrd[j], out=X[:, j, 0:N],
                                 in0=PS[pi][:, 0:N], in1=X[:, j, 0:N], op=ALU.add)
                        ps_free[pi] = [t]
                        xtok[j] = t
                        xrd[j] = []
                for k in range(KC):
                    br_rd[k] = [last_pe]

        def load_x(src_rows, N):
            nb = (N + 127) // 128
            for tb in range(nb):
                nt = min(128, N - tb * 128)
                yb = tb % 2
                t_ld = P.dma("sp", "xin%d" % yb, deps=yst_free[yb], out=YST[yb][0:nt, :],
                             in_=src_rows[tb * 128:tb * 128 + nt, :])
                t_tr = None
                for k4 in range(4):
                    pi = palloc()
                    for kk in range(4):
                        k = k4 * 4 + kk
                        t_tr = P.op("pe", "transpose", deps=[t_ld] + (ps_free[pi] if kk == 0 else []), sig=(kk == 3),
                                    out=PS[pi][:, kk * 128:kk * 128 + nt], in_=YST[yb][0:nt, k * 128:(k + 1) * 128],
                                    identity=CST[0:nt, 0:nt])
                    dw = []
                    for kk in range(4):
                        k = k4 * 4 + kk
                        dw += [xtok[k]] + xrd[k]
                    if nt == 128:
                        t_cp = P.op("act", "activation", deps=[t_tr] + dw,
                                    out=X[:, k4 * 4:(k4 + 1) * 4, tb * 128:(tb + 1) * 128],
                                    in_=PS[pi][:, :].rearrange("p (a n) -> p a n", a=4), func=AF.Copy)
                    else:
                        t_cp = P.op("act", "activation", deps=[t_tr] + dw,
                                    out=X[:, k4 * 4:(k4 + 1) * 4, 0:nt],
                                    in_=PS[pi][:, :].rearrange("p (a n) -> p a n", a=4)[:, :, 0:nt], func=AF.Copy)
                    ps_free[pi] = [t_cp]
                    for kk in range(4):
                        k = k4 * 4 + kk
                        xtok[k] = t_cp
                        xrd[k] = []
                yst_free[yb] = [t_tr]

        def store_y(dst_rows, N):
            nonlocal xn_rd
            sq = []
            for k in range(KC):
                t = P.op("act", "activation", deps=[xtok[k]] + xn_rd, out=XN[:, k, 0:N],
                         in_=X[:, k, 0:N], func=AF.Square)
                xrd[k].append(t)
                sq.append(t)
            pi = palloc()
            t_mm = mm_group(pi, N, [(ONES[:, :], XN[:, k, 0:N]) for k in range(KC)], sq)
            xn_rd = [t_mm]
            ti = talloc()
            t_rt = P.op("act", "activation", deps=[t_mm] + tmp_free[ti], out=TMP[ti][:, 0:N],
                        in_=PS[pi][:, 0:N], func=AF.Sqrt, scale=1.0 / D, bias=EPS)
            ps_free[pi] = [t_rt]
            t_rs = P.op("dve", "reciprocal", deps=[t_rt], out=RSTD[:, 0:N], in_=TMP[ti][:, 0:N])
            tmp_free[ti] = [t_rs]
            nb = (N + 127) // 128
            for tb in range(nb):
                nt = min(128, N - tb * 128)
                yb = tb % 2
                cps = []
                for k4 in range(4):
                    ta = talloc()
                    pi = palloc()
                    t_tr = None
                    tn_all = []
                    for kk in range(4):
                        k = k4 * 4 + kk
                        t_n = P.op("dve", "scalar_tensor_tensor", deps=[t_rs, xtok[k]] + (tmp_free[ta] if kk == 0 else []),
                                   out=TMP[ta][:, kk * 128:kk * 128 + nt], in0=X[:, k, tb * 128:tb * 128 + nt],
                                   scalar=PV[:, PV_GFINAL + k:PV_GFINAL + k + 1], in1=RSTD[:, tb * 128:tb * 128 + nt],
                                   op0=ALU.mult, op1=ALU.mult)
                        xrd[k].append(t_n)
                        tn_all.append(t_n)
                        t_tr = P.op("pe", "transpose", deps=[t_n] + (ps_free[pi] if kk == 0 else []), sig=(kk == 3),
                                    out=PS[pi][0:nt, kk * 128:(kk + 1) * 128], in_=TMP[ta][:, kk * 128:kk * 128 + nt],
                                    identity=IDENT)
                    tmp_free[ta] = [t_tr]
                    t_cp = P.op("act", "activation", deps=[t_tr] + (yst_free[yb] if k4 == 0 else []),
                                out=YST[yb][0:nt, k4 * 512:(k4 + 1) * 512], in_=PS[pi][0:nt, :], func=AF.Copy)
                    ps_free[pi] = [t_cp]
                    cps.append(t_cp)
                t_st = P.dma("sp", "yout%d" % yb, deps=cps, out=dst_rows[tb * 128:tb * 128 + nt, :],
                             in_=YST[yb][0:nt, :])
                yst_free[yb] = [t_st]
                out_toks.append(t_st)

        passes = []
        for ti_idx in range(n_ptiles):
            passes.append(("p", ti_idx))
        if do_sample:
            passes.append(("s", 0))
        state["npass"] = len(passes)
        for pidx, (mode, ti_idx) in enumerate(passes):
            state["pidx"] = pidx
            state["mcol"] = 0 if mode == "p" else 1
            if mode == "s":
                N = NS
                load_x(xs_d, N)
            else:
                N = TW
                load_x(xp_d[ti_idx * TW:(ti_idx + 1) * TW, :], N)
            for l in range(depth):
                state["l"], state["ul"] = l, 0
                layer(l, N, mode, ti_idx)
            if mode == "s":
                store_y(ys_d, N)
            else:
                store_y(yp_d[ti_idx * TW:(ti_idx + 1) * TW, :], N)

        def tr_out(src_ap, n, dst_ap, deps):
            pi = palloc()
            t_tr = P.op("pe", "transpose", deps=deps + ps_free[pi], out=PS[pi][0:n, 0:128], in_=src_ap, identity=IDENT)
            t_cp = P.op("act", "activation", deps=[t_tr] + ost_free, out=dst_ap, in_=PS[pi][0:n, 0:128], func=AF.Copy)
            ps_free[pi] = [t_cp]
            return t_cp

        ost_free = list(yst_free[1])
        for l in range(depth):
            if n_ptiles > 0:
                for name, carry, H, nch, dst in (("UPX", CP, 15, 4, npp_d), ("ULX", CL, 3, 8, nlcp_d),
                                                 ("VX", CV, 30, 4, nccp_d)):
                    cps = [tr_out(carry[:, l, c, :], H, OST[0:H, c * 128:(c + 1) * 128], [carry_tok[(name, l)]])
                           for c in range(nch)]
                    t_st = P.dma("sp", "ost", deps=cps, out=dst[l], in_=OST[0:H, 0:nch * 128])
                    ost_free = [t_st]
                    out_toks.append(t_st)
                cp = tr_out(CH[:, l * 8:(l + 1) * 8], 8, OST[0:8, 0:128], [carry_tok[("h", l, c)] for c in range(8)])
                t_st = P.dma("sp", "ost", deps=[cp], out=nlhp_d[l], in_=OST[0:8, 0:128])
                ost_free = [t_st]
                out_toks.append(t_st)
            if do_sample:
                for name, src, nch, dst, row in (("sUPX", SCP, 4, nps_d, 14), ("sULX", SCL, 8, nlcs_d, 2),
                                                 ("sVX", SCV, 4, nccs_d, 29)):
                    cps = [tr_out(src[:, l, c, :], NS, OST[0:NS, c * 128:(c + 1) * 128], [carry_tok[(name, l)]])
                           for c in range(nch)]
                    t_st = P.dma("sp", "ost", deps=cps, out=dst[l][:, row, :], in_=OST[0:NS, 0:nch * 128])
                    ost_free = [t_st]
                    out_toks.append(t_st)
                cps = [tr_out(SCH[:, l, c, :], NS, OST[0:NS, c * 128:(c + 1) * 128], [carry_tok[("sh", l, c)]])
                       for c in range(8)]
                t_st = P.dma("sp", "ost", deps=cps, out=nlhs_d[l], in_=OST[0:NS, 0:1024])
                ost_free = [t_st]
                out_toks.append(t_st)

        final = {}
        for (sem, val) in out_toks:
            final[id(sem)] = (sem, max(val, final.get(id(sem), (sem, 0))[1]))
        P.seen = {}
        P.wait_only("sp", list(final.values()))

        with nc.Block() as block:
            @block.tensor
            def _(e):
                P.replay("pe", e)

            @block.scalar
            def _(e):
                P.replay("act", e)

            @block.vector
            def _(e):
                P.replay("dve", e)

            @block.gpsimd
            def _(e):
                P.replay("pool", e)

            @block.sync
            def _(e):
                P.replay("sp", e)
    return nc


_CACHE = {}


def kernel(**inputs):
    depth = int(os.environ.get("KDEPTH", L))
    n_ptiles = int(os.environ.get("KTILES", SEQ // TW))
    do_sample = os.environ.get("KSAMPLE", "1") == "1"
    f = lambda a: np.ascontiguousarray(np.asarray(a, dtype=np.float32))
    inp = {k: f(v) for k, v in inputs.items()}

    rows = []
    for l in range(L):
        for name, n in PV_ITEMS:
            a = inp[name][l].reshape(-1, 128)
            assert a.shape[0] == n, (name, a.shape)
            rows.append(a)
    rows.append(inp["g_final"].reshape(-1, 128))
    pvec = np.ascontiguousarray(np.concatenate(rows, axis=0))
    assert pvec.shape == (PV_COLS, 128)

    cst = np.zeros((128, 194), np.float32)
    cst[:, 193] = 1.0
    cst[:, 0:128] = np.eye(128, dtype=np.float32)
    for g, w in enumerate(POOL_W):
        for t in range(16):
            cst[:, 128 + g * 16 + t] = float(w) / float(min(w, t + 1))

    key = (depth, n_ptiles, do_sample)
    if key not in _CACHE:
        _CACHE[key] = build_program(depth, n_ptiles, do_sample)
    nc = _CACHE[key]

    shared = {"w_in": inp["w_in"], "w_gate": inp["w_gate"], "w_out": inp["w_out"], "w_up": inp["w_up"],
              "w_down": inp["w_down"], "w_pool_br": inp["w_pool_br"], "w_lru_br": inp["w_lru_br"],
              "w_conf_br": inp["w_conf_br"], "w_pool_grp": inp["w_pool_grp"], "w_lru_a": inp["w_lru_a"],
              "w_lru_x": inp["w_lru_x"], "pvec": pvec}
    RC = [0, 1, 4, 5]
    in_maps = []
    for c in range(8):
        m = dict(shared)
        cc = cst.copy()
        cc[:, 192] = 1.0 if c in RC else 0.0
        m["cst"] = cc
        m["xp"] = inp["x_prompt"][RC.index(c) if c in RC else c % 4]
        sl = slice(c * NS, (c + 1) * NS)
        m["xs"] = np.ascontiguousarray(inp["x_sample"][sl, 0, :])
        m["st_pool"] = np.ascontiguousarray(inp["state_pool"][:, sl])
        m["st_lconv"] = np.ascontiguousarray(inp["state_lru_conv"][:, sl])
        m["st_lh"] = np.ascontiguousarray(inp["state_lru_h"][:, sl])
        m["st_cconv"] = np.ascontiguousarray(inp["state_conf_conv"][:, sl])
        in_maps.append(m)

    res = run_bass_kernel_spmd(nc, in_maps, core_ids=list(range(8)))
    R = res.results
    y_prompt = np.stack([R[RC[b]]["yp"] for b in range(4)], axis=0)
    y_sample = np.concatenate([R[c]["ys"] for c in range(8)], axis=0)[:, None, :]
    npp = np.stack([R[RC[b]]["npp"] for b in range(4)], axis=1)
    nlcp = np.stack([R[RC[b]]["nlcp"] for b in range(4)], axis=1)
    nlhp = np.stack([R[RC[b]]["nlhp"].reshape(L, 1024) for b in range(4)], axis=1)
    nccp = np.stack([R[RC[b]]["nccp"] for b in range(4)], axis=1)
    nps = np.concatenate([R[c]["nps"] for c in range(8)], axis=1)
    nlcs = np.concatenate([R[c]["nlcs"] for c in range(8)], axis=1)
    nlhs = np.concatenate([R[c]["nlhs"] for c in range(8)], axis=1)
    nccs = np.concatenate([R[c]["nccs"] for c in range(8)], axis=1)
    outs = (y_prompt, y_sample, npp, nlcp, nlhp, nccp, nps, nlcs, nlhs, nccs)
    return tuple(np.ascontiguousarray(o, dtype=np.float32) for o in outs)
```
